# Optimizing a Trainium2 kernel written in Bass

```python
import math
import jax, jax.numpy as jnp
from jax import lax
import numpy as np

D_MODEL = 1024
BATCH = 4
SEQ = 4096
DEPTH = 4

N_MIXERS = 2
N_NSA_LAYERS = (DEPTH + 1) // 2
N_DIFF_LAYERS = DEPTH // 2
HEAD_DIM = 64
ROPE_DIM = HEAD_DIM // 4
ROPE_THETA = 500000.0
NSA_HEADS = D_MODEL // HEAD_DIM
NSA_GROUPS = 4
NSA_HPG = NSA_HEADS // NSA_GROUPS
CMP_BLOCK = 32
CMP_STRIDE = 16
CMP_HIDDEN = 2 * HEAD_DIM
SEL_BLOCK = 64
SEL_TOPK = 16
WINDOW = 512
NSA_Q_CHUNK = 64
NSA_Q_W = NSA_HEADS * HEAD_DIM
NSA_KV_W = NSA_GROUPS * HEAD_DIM
NSA_IN = NSA_Q_W + 6 * NSA_KV_W + 3 * NSA_HEADS
DIFF_HEADS = D_MODEL // (2 * HEAD_DIM)
DIFF_V_DIM = 2 * HEAD_DIM
DIFF_W = DIFF_HEADS * 2 * HEAD_DIM
DIFF_IN = 3 * DIFF_W
ATT_BLOCK = 128
D_FF = 4 * D_MODEL
EPS = 1e-6
NEG_INF = -1e30
SEL_FORCE = 1e6

kernel_name = "hybrid_nsa_diffattn_adaln_trunk"


def rms_norm(x, g):
    xf = x.astype(jnp.float32)
    y = xf * lax.rsqrt(jnp.mean(xf * xf, axis=-1, keepdims=True) + EPS)
    return (y * g.astype(jnp.float32)).astype(x.dtype)


def rope_tables(positions):
    inv = ROPE_THETA ** (-jnp.arange(0, ROPE_DIM, 2, dtype=jnp.float32) / ROPE_DIM)
    ang = positions.astype(jnp.float32)[..., None] * inv
    return jnp.cos(ang), jnp.sin(ang)


def apply_partial_rope(x, cos, sin):
    half = ROPE_DIM // 2
    shape = cos.shape[:2] + (1,) * (x.ndim - 3) + cos.shape[-1:]
    c = cos.reshape(shape)
    s = sin.reshape(shape)
    x1 = x[..., :half].astype(jnp.float32)
    x2 = x[..., half:ROPE_DIM].astype(jnp.float32)
    rot = jnp.concatenate([x1 * c - x2 * s, x2 * c + x1 * s], axis=-1).astype(x.dtype)
    return jnp.concatenate([rot, x[..., ROPE_DIM:]], axis=-1)


def masked_softmax(s, mask):
    return jax.nn.softmax(jnp.where(mask, s, NEG_INF), axis=-1) * mask


def compress_blocks(blk, pe, w1, w2):
    B, NC = blk.shape[:2]
    z = (blk + pe[:, None, :]).transpose(0, 1, 3, 2, 4).reshape(B, NC, NSA_GROUPS, CMP_BLOCK * HEAD_DIM)
    return jax.nn.silu(z @ w1) @ w2


def nsa_mixer(h, cos, sin, w_in, b_gate, q_gain, k_gain, pe_k, w_ck1, w_ck2, pe_v, w_cv1, w_cv2, w_out):
    B, S, _ = h.shape
    G, HPG, dh = NSA_GROUPS, NSA_HPG, HEAD_DIM
    scale = 1.0 / math.sqrt(dh)
    sizes = [NSA_Q_W] + [NSA_KV_W] * 6 + [3 * NSA_HEADS]
    q, kc, vc, ks, vs, kw, vw, gl = jnp.split(h @ w_in, list(np.cumsum(sizes)[:-1]), axis=-1)
    q = rms_norm(q.reshape(B, S, NSA_HEADS, dh), q_gain)
    q_rot = apply_partial_rope(q, cos, sin)
    q_cmp = q.reshape(B, S, G, HPG, dh).transpose(0, 2, 3, 1, 4)
    q_rot = q_rot.reshape(B, S, G, HPG, dh).transpose(0, 2, 3, 1, 4)
    gates = jax.nn.sigmoid(gl.astype(jnp.float32) + b_gate.astype(jnp.float32)).reshape(B, S, NSA_HEADS, 3)

    n_cmp = (S - CMP_BLOCK) // CMP_STRIDE + 1
    idx = jnp.arange(n_cmp)[:, None] * CMP_STRIDE + jnp.arange(CMP_BLOCK)[None, :]
    kc = kc.reshape(B, S, G, dh)
    vc = vc.reshape(B, S, G, dh)
    k_cmp = rms_norm(compress_blocks(kc[:, idx], pe_k, w_ck1, w_ck2), k_gain[0]).transpose(0, 2, 1, 3)
    v_cmp = compress_blocks(vc[:, idx], pe_v, w_cv1, w_cv2).transpose(0, 2, 1, 3)
    cmp_start = jnp.arange(n_cmp) * CMP_STRIDE
    cmp_end = cmp_start + CMP_BLOCK - 1
    n_sel = S // SEL_BLOCK
    top_n = min(SEL_TOPK, n_sel)
    sel_start = jnp.arange(n_sel) * SEL_BLOCK
    overlap = ((cmp_start[:, None] <= sel_start[None, :] + SEL_BLOCK - 1)
               & (cmp_end[:, None] >= sel_start[None, :])).astype(jnp.float32)

    ks = apply_partial_rope(rms_norm(ks.reshape(B, S, G, dh), k_gain[1]), cos, sin)
    ks_blk = ks.transpose(0, 2, 1, 3).reshape(B, G, n_sel, SEL_BLOCK, dh)
    vs_blk = vs.reshape(B, S, G, dh).transpose(0, 2, 1, 3).reshape(B, G, n_sel, SEL_BLOCK, dh)
    kw = apply_partial_rope(rms_norm(kw.reshape(B, S, G, dh), k_gain[2]), cos, sin)
    pad = ((0, 0), (0, 0), (WINDOW, 0), (0, 0))
    kw_pad = jnp.pad(kw.transpose(0, 2, 1, 3), pad)
    vw_pad = jnp.pad(vw.reshape(B, S, G, dh).transpose(0, 2, 1, 3), pad)
    b_idx = jnp.arange(B)[:, None, None, None]
    g_idx = jnp.arange(G)[None, :, None, None]
    blk_ids = jnp.arange(n_sel)
    QC = NSA_Q_CHUNK

    def chunk(ci):
        s0 = ci * QC
        t = s0 + jnp.arange(QC)
        qc_ = lax.dynamic_slice_in_dim(q_cmp, s0, QC, axis=3)
        qr_ = lax.dynamic_slice_in_dim(q_rot, s0, QC, axis=3)
        sc = jnp.einsum('bghqd,bgcd->bghqc', qc_, k_cmp).astype(jnp.float32) * scale
        pc = masked_softmax(sc, cmp_end[None, :] <= t[:, None])
        o_cmp = jnp.einsum('bghqc,bgcd->bghqd', pc.astype(v_cmp.dtype), v_cmp)
        imp = jnp.einsum('bghqc,cs->bgqs', pc, overlap)
        bt = (t // SEL_BLOCK)[:, None]
        valid = blk_ids[None, :] <= bt
        forced = (blk_ids[None, :] == 0) | (blk_ids[None, :] == bt) | (blk_ids[None, :] == bt - 1)
        imp = jnp.where(forced, SEL_FORCE, jnp.where(valid, imp, -1.0))
        _, sel = lax.top_k(imp, top_n)
        kg = ks_blk[b_idx, g_idx, sel]
        vg = vs_blk[b_idx, g_idx, sel]
        ss = jnp.einsum('bghqd,bgqnld->bghqnl', qr_, kg).astype(jnp.float32) * scale
        tok = sel[..., None] * SEL_BLOCK + jnp.arange(SEL_BLOCK)
        ms = (tok <= t[:, None, None])[:, :, None]
        shp = ss.shape
        ps = masked_softmax(ss.reshape(shp[:4] + (top_n * SEL_BLOCK,)),
                            ms.reshape(ms.shape[:4] + (top_n * SEL_BLOCK,))).reshape(shp)
        o_sel = jnp.einsum('bghqnl,bgqnld->bghqd', ps.astype(vg.dtype), vg)
        kw_ = lax.dynamic_slice_in_dim(kw_pad, s0, WINDOW + QC, axis=2)
        vw_ = lax.dynamic_slice_in_dim(vw_pad, s0, WINDOW + QC, axis=2)
        kpos = s0 - WINDOW + jnp.arange(WINDOW + QC)
        mw = (kpos[None, :] <= t[:, None]) & (kpos[None, :] > t[:, None] - WINDOW) & (kpos[None, :] >= 0)
        sw = jnp.einsum('bghqd,bgkd->bghqk', qr_, kw_).astype(jnp.float32) * scale
        o_win = jnp.einsum('bghqk,bgkd->bghqd', masked_softmax(sw, mw).astype(vw_.dtype), vw_)
        g = lax.dynamic_slice_in_dim(gates, s0, QC, axis=1)
        g = g.reshape(B, QC, G, HPG, 3).transpose(0, 2, 3, 1, 4).astype(o_win.dtype)
        o = g[..., 0:1] * o_cmp + g[..., 1:2] * o_sel + g[..., 2:3] * o_win
        return o.transpose(0, 3, 1, 2, 4).reshape(B, QC, NSA_Q_W)

    outs = lax.map(chunk, jnp.arange(S // QC))
    return outs.transpose(1, 0, 2, 3).reshape(B, S, NSA_Q_W) @ w_out


def diff_mixer(h, cos, sin, layer_depth, w_in, q_gain, k_gain, lq1, lk1, lq2, lk2, subln_g, w_out):
    B, S, _ = h.shape
    dh = HEAD_DIM
    scale = 1.0 / math.sqrt(dh)
    q, k, v = jnp.split(h @ w_in, [DIFF_W, 2 * DIFF_W], axis=-1)
    q = apply_partial_rope(rms_norm(q.reshape(B, S, DIFF_HEADS, 2, dh), q_gain), cos, sin)
    k = apply_partial_rope(rms_norm(k.reshape(B, S, DIFF_HEADS, 2, dh), k_gain), cos, sin)
    q = q.transpose(0, 2, 3, 1, 4)
    k = k.transpose(0, 2, 3, 1, 4)
    v = v.reshape(B, S, DIFF_HEADS, DIFF_V_DIM).transpose(0, 2, 1, 3)
    lam_init = 0.8 - 0.6 * math.exp(-0.3 * (layer_depth - 1))
    lam = (jnp.exp(jnp.sum(lq1.astype(jnp.float32) * lk1.astype(jnp.float32)))
           - jnp.exp(jnp.sum(lq2.astype(jnp.float32) * lk2.astype(jnp.float32))) + lam_init)
    kpos = jnp.arange(S)

    def block(bi):
        s0 = bi * ATT_BLOCK
        qb = lax.dynamic_slice_in_dim(q, s0, ATT_BLOCK, axis=3)
        s = jnp.einsum('bhcqd,bhckd->bhcqk', qb, k).astype(jnp.float32) * scale
        t = s0 + jnp.arange(ATT_BLOCK)
        p = masked_softmax(s, kpos[None, :] <= t[:, None])
        a = p[:, :, 0] - lam * p[:, :, 1]
        o = jnp.einsum('bhqk,bhkd->bhqd', a.astype(v.dtype), v)
        o = rms_norm(o, subln_g) * (1.0 - lam_init)
        return o.transpose(0, 2, 1, 3).reshape(B, ATT_BLOCK, DIFF_W)

    outs = lax.map(block, jnp.arange(S // ATT_BLOCK))
    return outs.transpose(1, 0, 2, 3).reshape(B, S, DIFF_W) @ w_out


def setup_inputs(seed: int = 0) -> dict:
    key = jax.random.key(seed)
    ks = iter(jax.random.split(key, 32))
    D = D_MODEL

    def nrm(shape, scale):
        return jax.random.normal(next(ks), shape, jnp.float32) * scale

    def gain(shape):
        return 1.0 + nrm(shape, 0.02)

    x = nrm((BATCH, SEQ, D), 1.0)
    c = nrm((BATCH, D), 1.0)
    offsets = jax.random.randint(next(ks), (BATCH, 1), 0, 1024, dtype=jnp.int32)
    positions = (offsets + jnp.arange(SEQ, dtype=jnp.int32)[None, :]).astype(jnp.int32)
    NL, NDF = N_NSA_LAYERS, N_DIFF_LAYERS
    return {
        'x': x, 'c': c, 'positions': positions,
        'ln_mix_g': gain((DEPTH, D)), 'ln_mlp_g': gain((DEPTH, D)),
        'w_ada': nrm((DEPTH, D, 6 * D), D ** -0.5), 'b_ada': nrm((DEPTH, 6 * D), 0.01),
        'w_mlp_in': nrm((DEPTH, D, D_FF), D ** -0.5), 'w_mlp_out': nrm((DEPTH, D_FF, D), D_FF ** -0.5),
        'nsa_w_in': nrm((NL, D, NSA_IN), D ** -0.5), 'nsa_b_gate': nrm((NL, 3 * NSA_HEADS), 0.01),
        'nsa_q_gain': gain((NL, HEAD_DIM)), 'nsa_k_gain': gain((NL, 3, HEAD_DIM)),
        'nsa_pe_k': nrm((NL, CMP_BLOCK, HEAD_DIM), 0.1),
        'nsa_w_ck1': nrm((NL, CMP_BLOCK * HEAD_DIM, CMP_HIDDEN), (CMP_BLOCK * HEAD_DIM) ** -0.5),
        'nsa_w_ck2': nrm((NL, CMP_HIDDEN, HEAD_DIM), CMP_HIDDEN ** -0.5),
        'nsa_pe_v': nrm((NL, CMP_BLOCK, HEAD_DIM), 0.1),
        'nsa_w_cv1': nrm((NL, CMP_BLOCK * HEAD_DIM, CMP_HIDDEN), (CMP_BLOCK * HEAD_DIM) ** -0.5),
        'nsa_w_cv2': nrm((NL, CMP_HIDDEN, HEAD_DIM), CMP_HIDDEN ** -0.5),
        'nsa_w_out': nrm((NL, NSA_Q_W, D), NSA_Q_W ** -0.5),
        'diff_w_in': nrm((NDF, D, DIFF_IN), D ** -0.5),
        'diff_q_gain': gain((NDF, HEAD_DIM)), 'diff_k_gain': gain((NDF, HEAD_DIM)),
        'diff_lq1': nrm((NDF, HEAD_DIM), 0.1), 'diff_lk1': nrm((NDF, HEAD_DIM), 0.1),
        'diff_lq2': nrm((NDF, HEAD_DIM), 0.1), 'diff_lk2': nrm((NDF, HEAD_DIM), 0.1),
        'diff_subln_g': gain((NDF, DIFF_V_DIM)),
        'diff_w_out': nrm((NDF, DIFF_W, D), DIFF_W ** -0.5),
    }


def reference(x, c, positions, ln_mix_g, ln_mlp_g, w_ada, b_ada, w_mlp_in, w_mlp_out,
              nsa_w_in, nsa_b_gate, nsa_q_gain, nsa_k_gain, nsa_pe_k, nsa_w_ck1, nsa_w_ck2,
              nsa_pe_v, nsa_w_cv1, nsa_w_cv2, nsa_w_out,
              diff_w_in, diff_q_gain, diff_k_gain, diff_lq1, diff_lk1, diff_lq2, diff_lk2,
              diff_subln_g, diff_w_out):
    cos, sin = rope_tables(positions)
    cond = jax.nn.silu(c)
    for i in range(DEPTH):
        mod = cond @ w_ada[i] + b_ada[i]
        sh1, sc1, g1, sh2, sc2, g2 = jnp.split(mod, 6, axis=-1)
        h = rms_norm(x, ln_mix_g[i]) * (1.0 + sc1[:, None, :]) + sh1[:, None, :]
        j = i // N_MIXERS
        if i % N_MIXERS == 0:
            y = nsa_mixer(h, cos, sin, nsa_w_in[j], nsa_b_gate[j], nsa_q_gain[j], nsa_k_gain[j],
                          nsa_pe_k[j], nsa_w_ck1[j], nsa_w_ck2[j], nsa_pe_v[j], nsa_w_cv1[j],
                          nsa_w_cv2[j], nsa_w_out[j])
        else:
            y = diff_mixer(h, cos, sin, i + 1, diff_w_in[j], diff_q_gain[j], diff_k_gain[j],
                           diff_lq1[j], diff_lk1[j], diff_lq2[j], diff_lk2[j], diff_subln_g[j],
                           diff_w_out[j])
        x = x + g1[:, None, :] * y
        h = rms_norm(x, ln_mlp_g[i]) * (1.0 + sc2[:, None, :]) + sh2[:, None, :]
        x = x + g2[:, None, :] * (jnp.square(jax.nn.relu(h @ w_mlp_in[i])) @ w_mlp_out[i])
    return x
```

```python
import math
import numpy as np
import concourse.bass as bass
import concourse.mybir as mybir
from concourse.bass_utils import run_bass_kernel_spmd

F32 = mybir.dt.float32
BF16 = mybir.dt.bfloat16
I32 = mybir.dt.int32
AF = mybir.ActivationFunctionType
ALU = mybir.AluOpType
AX = mybir.AxisListType

EPOCH = 30000


class Dep:
    __slots__ = ("w", "rs")

    def __init__(self):
        self.w = None
        self.rs = {}


class Buf:
    def __init__(self, t, nreg=1):
        self.t = t
        self.d = [Dep() for _ in range(nreg)]

    def __getitem__(self, idx):
        return self.t[idx]

    @property
    def all(self):
        return self.d


class FW:
    def __init__(self, nc):
        self.nc = nc
        self.q = {"pe": nc.tensor, "act": nc.scalar, "dve": nc.vector, "pool": nc.gpsimd, "sp": nc.sync}
        self.sem = {}
        self.cnt = {}
        self.seen = {e: {} for e in self.q}
        self.nsem = 0
        for e in self.q:
            self._new_epoch(e)
        self.ring = {}
        self.ring_val = {}
        self.ring_last = {}
        self.ring_n = {}
        for e, n in (("sp", 16), ("pool", 12), ("act", 6)):
            self.ring[e] = [self._alloc_sem(f"dma_{e}_{i}") for i in range(n)]
            self.ring_val[e] = [0] * n
            self.ring_last[e] = [None] * n
            self.ring_n[e] = 0
        self.out_events = []
        self.nbuf = 0
        self.psum_banks = None
        self.psum_i = 0
        self.ninst = 0

    def _alloc_sem(self, name):
        self.nsem += 1
        return self.nc.alloc_semaphore(f"{name}_{self.nsem}")

    def _new_epoch(self, e):
        self.sem[e] = self._alloc_sem(f"eng_{e}")
        self.cnt[e] = 0

    def sbuf(self, shape, dtype, nreg=1, name=None):
        self.nbuf += 1
        t = self.nc.alloc_sbuf_tensor(f"{name or 'sb'}_{self.nbuf}", list(shape), dtype)
        return Buf(t, nreg)

    def view(self, ap, olds=(), nreg=1):
        b = Buf(ap, nreg)
        k = 0
        for o in olds:
            for od in o.d:
                evs = list(od.rs.values()) + ([od.w] if od.w is not None else [])
                for ev in evs:
                    for d in b.d:
                        d.rs[("inh", k)] = ev
                    k += 1
        return b

    def init_psum(self):
        self.psum_banks = []
        for i in range(8):
            t = self.nc.alloc_psum_tensor(f"psb{i}", [128, 512], F32)
            self.psum_banks.append(Buf(t, 1))

    def psum(self):
        b = self.psum_banks[self.psum_i % 8]
        self.psum_i += 1
        return b

    def _wait(self, e, ev):
        if ev is None:
            return
        sem, val, owner = ev
        if owner == e and e in ("pe", "sp"):
            return
        key = sem.num if hasattr(sem, "num") else id(sem)
        if self.seen[e].get(key, 0) >= val:
            return
        self.q[e].wait_ge(sem, val)
        self.seen[e][key] = val

    def _pre(self, e, reads, writes):
        for d in reads:
            self._wait(e, d.w)
        for d in writes:
            if d.w is not None and d.w[2] != e:
                self._wait(e, d.w)
            elif d.w is not None and d.w[2] == e and e not in ("pe",):
                pass
            for own, ev in d.rs.items():
                if isinstance(own, str) and own == e:
                    continue
                self._wait(e, ev)

    def _post(self, ev, reads, writes):
        own = ev[2]
        for d in reads:
            if own == "dma":
                d.rs[("dma", id(ev[0]), ev[1])] = ev
            else:
                d.rs[own] = ev
        for d in writes:
            d.w = ev
            d.rs = {}

    def op(self, e, fn, reads=(), writes=()):
        self._pre(e, reads, writes)
        ins = fn(self.q[e])
        if self.cnt[e] >= EPOCH:
            self._new_epoch(e)
        self.cnt[e] += 1
        ins.then_inc(self.sem[e], 1)
        ev = (self.sem[e], self.cnt[e], e)
        self._post(ev, reads, writes)
        self.ninst += 1
        return ev

    def dma(self, e, out, in_, reads=(), writes=(), is_output=False, **kw):
        n = self.ring_n[e]
        slot = n % len(self.ring[e])
        self.ring_n[e] += 1
        prev = self.ring_last[e][slot]
        if prev is not None:
            sem, val, _ = prev
            key = sem.num if hasattr(sem, "num") else id(sem)
            if self.seen[e].get(key, 0) < val:
                self.q[e].wait_ge(sem, val)
                self.seen[e][key] = val
        self._pre(e, reads, writes)
        ins = self.q[e].dma_start(out=out, in_=in_, **kw)
        self.ring_val[e][slot] += 16
        ins.then_inc(self.ring[e][slot], 16)
        ev = (self.ring[e][slot], self.ring_val[e][slot], "dma")
        self.ring_last[e][slot] = ev
        self._post(ev, reads, writes)
        if is_output:
            self.out_events.append(ev)
        self.ninst += 1
        return ev

    def finish(self):
        for e in self.ring:
            for ev in self.ring_last[e]:
                if ev is not None:
                    sem, val, _ = ev
                    key = sem.num if hasattr(sem, "num") else id(sem)
                    if self.seen["sp"].get(key, 0) < val:
                        self.q["sp"].wait_ge(sem, val)
                        self.seen["sp"][key] = val


D = 1024
DFF = 4096
EPS = 1e-6


def mod_cols(fw, sc_bf32, w_ada, b_ada_cols, col0, ncols, out_buf, wtmp, plus_one_cols=()):
    nc = fw.nc
    ps = fw.psum()
    for j in range(ncols):
        c = col0 + j
        wt = wtmp[j % 2]
        fw.dma("sp", wt[:, :, :], w_ada[:, c * 128:(c + 1) * 128].rearrange("(kc p) n -> p kc n", p=128),
               writes=wt.all)
        for kc in range(8):
            fw.op("pe", lambda e, kc=kc, wt=wt, j=j: e.matmul(ps[:, j:j + 1], wt[:, kc, :], sc_bf32[:, kc:kc + 1],
                                                             start=(kc == 0), stop=(kc == 7)),
                  reads=wt.all + sc_bf32.all, writes=ps.all)
    fw.op("dve", lambda e: e.tensor_tensor(out_buf[:, 0:ncols], ps[:, 0:ncols], b_ada_cols[:, col0:col0 + ncols], ALU.add),
          reads=ps.all + b_ada_cols.all, writes=out_buf.all)


def mod_bcast(fw, screp, w_ada, b_ada_row, col0, out_buf, wtmp):
    for half in range(2):
        ps = fw.psum()
        c0 = col0 * 128 + half * 512
        wt = wtmp[half % 2]
        fw.dma("sp", wt[:, :, :], w_ada[:, c0:c0 + 512].rearrange("(kc p) n -> p kc n", p=128), writes=wt.all)
        for kc in range(8):
            fw.op("pe", lambda e, kc=kc, wt=wt: e.matmul(ps[:, :], screp[:, kc, :], wt[:, kc, :],
                                                        start=(kc == 0), stop=(kc == 7)),
                  reads=wt.all + screp.all, writes=ps.all)
        fw.dma("sp", out_buf[:, half * 512:(half + 1) * 512],
               b_ada_row[0:1, c0:c0 + 512].partition_broadcast(128), writes=out_buf.all)
        fw.op("dve", lambda e, half=half, ps=ps: e.tensor_tensor(out_buf[:, half * 512:(half + 1) * 512], ps[:, :],
                                                               out_buf[:, half * 512:(half + 1) * 512], ALU.add),
              reads=ps.all + out_buf.all, writes=out_buf.all)


def build_mlp(TOK=2048, CH=256):
    nc = bass.Bass("TRN2", target_bir_lowering=False)
    dt = lambda name, shape, kind="ExternalInput", dtype=F32: nc.dram_tensor(name, shape, dtype, kind=kind).ap()
    x = dt("x", [TOK, D])
    oT = dt("oT", [D, TOK])
    cT = dt("cT", [128, 8])
    w_ada = dt("w_ada", [D, 6 * D])
    b_cols = dt("b_cols", [128, 48])
    b_row = dt("b_row", [1, 6 * D])
    g_cols = dt("g_cols", [128, 8])
    w_out = dt("w_out", [D, D])
    w_in = dt("w_in", [D, DFF])
    w_o2 = dt("w_o2", [DFF, D])
    ident = dt("ident", [128, 128])
    y = dt("y", [TOK, D], kind="ExternalOutput")

    fw = FW(nc)
    fw.init_psum()
    NT = CH // 128

    idb = fw.sbuf([128, 128], BF16, name="idb")
    fw.dma("pool", idb[:, :], ident[:, :], writes=idb.all)
    ct = fw.sbuf([128, 8], F32, name="ct")
    fw.dma("sp", ct[:, :], cT[:, :], writes=ct.all)
    bcols = fw.sbuf([128, 48], F32, name="bcols")
    fw.dma("sp", bcols[:, :], b_cols[:, :], writes=bcols.all)
    gcols = fw.sbuf([128, 8], F32, name="gcols")
    fw.dma("sp", gcols[:, :], g_cols[:, :], writes=gcols.all)
    sc = fw.sbuf([128, 8], F32, name="sc")
    fw.op("act", lambda e: e.activation(sc[:, :], ct[:, :], AF.Silu), reads=ct.all, writes=sc.all)
    screp = fw.sbuf([128, 8, 128], F32, name="screp")
    fw.op("dve", lambda e: e.tensor_copy(screp[:, :, :], sc[:, :].unsqueeze(2).to_broadcast([128, 8, 128])),
          reads=sc.all, writes=screp.all)

    w_in_sb = fw.sbuf([128, 8, DFF], BF16, nreg=8, name="w_in")
    w_o2_sb = fw.sbuf([128, 32, D], BF16, nreg=8, name="w_o2")
    w_out_sb = fw.sbuf([128, 8, D], BF16, nreg=1, name="w_out")
    fw.dma("pool", w_out_sb[:, :, :], w_out.rearrange("(kc p) n -> p kc n", p=128), writes=w_out_sb.all)

    scratch = [fw.sbuf([128, 4096], F32, name="scratch") for _ in range(2)]
    wtmp = [Buf(s.t[:, :].rearrange("p (kc n) -> p kc n", kc=8), 1) for s in scratch]
    g1b = fw.sbuf([128, D], F32, name="g1b")
    g2b = fw.sbuf([128, D], F32, name="g2b")
    mod_bcast(fw, screp, w_ada, b_row, 16, g1b, wtmp)
    mod_bcast(fw, screp, w_ada, b_row, 40, g2b, wtmp)
    modc = fw.sbuf([128, 16], F32, name="modc")

    class _V:
        def __init__(self, b):
            self.b = b
            self.all = b.all

        def __getitem__(self, idx):
            return self.b.t[:, :, 0:128][idx]
    mod_cols(fw, sc, w_ada, bcols, 24, 16, modc, [_V(wtmp[0]), _V(wtmp[1])])
    geff = fw.sbuf([128, 8], F32, name="geff")
    fw.op("dve", lambda e: e.scalar_tensor_tensor(geff[:, :], modc[:, 8:16], 1.0, gcols[:, :], ALU.add, ALU.mult),
          reads=modc.all + gcols.all, writes=geff.all)

    for r in range(8):
        fw.dma("pool", w_in_sb[:, r, :], w_in[r * 128:(r + 1) * 128, :], writes=[w_in_sb.d[r]])
    for r in range(8):
        fw.dma("pool", w_o2_sb[:, 4 * r:4 * r + 4, :],
               w_o2[r * 512:(r + 1) * 512, :].rearrange("(fc p) n -> p fc n", p=128), writes=[w_o2_sb.d[r]])

    oT_sb = [fw.sbuf([128, 8, CH], BF16, name="oT"),
             fw.view(screp.t[:, :, :].rearrange("p a b -> p (a b)").bitcast(BF16).rearrange("p (k t) -> p k t", k=8), [screp])]
    x_sb = [fw.sbuf([128, NT, D], F32, name="x"),
            fw.view(scratch[1].t[:, 0:2048].rearrange("p (n d) -> p n d", n=NT), [wtmp[1]])]
    xn_sb = fw.view(scratch[1].t[:, 2048:3072].bitcast(BF16).rearrange("p (n d) -> p n d", n=NT), [wtmp[1]])
    h2T = fw.view(scratch[1].t[:, 3072:4096].bitcast(BF16).rearrange("p (k t) -> p k t", k=8), [wtmp[1]])
    uT = fw.view(scratch[0].t[:, :].bitcast(BF16).rearrange("p (f t) -> p f t", f=32), [wtmp[0]])
    rl = [fw.sbuf([128, CH], F32, name="rl") for _ in range(2)]
    ss = fw.sbuf([128, 8], F32, name="ss")
    tmpf = fw.sbuf([128, 512], F32, name="tmpf")

    nchunks = TOK // CH
    for ci in range(nchunks):
        t0 = ci * CH
        ob = oT_sb[ci % 2]
        xb = x_sb[ci % 2]
        fw.dma("pool", ob[:, :, :], oT[:, t0:t0 + CH].rearrange("(kc p) t -> p kc t", p=128), writes=ob.all)
        fw.dma("sp", xb[:, :, :], x[t0:t0 + CH, :].rearrange("(n p) d -> p n d", p=128), writes=xb.all)
        fw.op("dve", lambda e: e.memset(ss[:, :], 0.0), writes=ss.all)
        for n in range(NT):
            for half in range(2):
                ps = fw.psum()
                for kc in range(8):
                    fw.op("pe", lambda e, kc=kc, n=n, half=half, ps=ps: e.matmul(
                        ps[:, :], ob[:, kc, n * 128:(n + 1) * 128], w_out_sb[:, kc, half * 512:(half + 1) * 512],
                        start=(kc == 0), stop=(kc == 7)), reads=ob.all + w_out_sb.all, writes=ps.all)
                sl = slice(half * 512, (half + 1) * 512)
                fw.op("dve", lambda e, ps=ps, sl=sl: e.tensor_tensor(tmpf[:, :], ps[:, :], g1b[:, sl], ALU.mult),
                      reads=ps.all + g1b.all, writes=tmpf.all)
                fw.op("dve", lambda e, n=n, sl=sl: e.tensor_tensor(xb[:, n, sl], tmpf[:, :], xb[:, n, sl], ALU.add),
                      reads=tmpf.all + xb.all, writes=xb.all)
            fw.op("act", lambda e, n=n: e.activation(xn_sb[:, n, :], xb[:, n, :], AF.Square, accum_out=ss[:, n:n + 1]),
                  reads=xb.all, writes=xn_sb.all + ss.all)
        fw.op("dve", lambda e: e.tensor_scalar(ss[:, 4:4 + NT], ss[:, 0:NT], 1.0 / D, EPS, ALU.mult, ALU.add),
              reads=ss.all, writes=ss.all)
        fw.op("act", lambda e: e.activation(ss[:, 4:4 + NT], ss[:, 4:4 + NT], AF.Sqrt), reads=ss.all, writes=ss.all)
        fw.op("dve", lambda e: e.reciprocal(ss[:, 4:4 + NT], ss[:, 4:4 + NT]), reads=ss.all, writes=ss.all)
        for n in range(NT):
            fw.op("dve", lambda e, n=n: e.tensor_scalar(xn_sb[:, n, :], xb[:, n, :], ss[:, 4 + n:5 + n], None, ALU.mult),
                  reads=xb.all + ss.all, writes=xn_sb.all)
        for kc in range(8):
            ps = fw.psum()
            psb = ps.t[:, :].bitcast(BF16)
            for n in range(NT):
                fw.op("pe", lambda e, kc=kc, n=n, psb=psb: e.transpose(psb[:, n * 128:(n + 1) * 128],
                                                                      xn_sb[:, n, kc * 128:(kc + 1) * 128], idb[:, :]),
                      reads=xn_sb.all + idb.all, writes=ps.all)
            fw.op("act", lambda e, kc=kc, psb=psb: e.activation(h2T[:, kc, :], psb[:, 0:CH], AF.Identity,
                                                               bias=modc[:, kc:kc + 1], scale=geff[:, kc:kc + 1]),
                  reads=ps.all + modc.all + geff.all, writes=h2T.all)
        for fc in range(32):
            ps = fw.psum()
            for kc in range(8):
                fw.op("pe", lambda e, kc=kc, fc=fc, ps=ps: e.matmul(ps[:, 0:CH], w_in_sb[:, kc, fc * 128:(fc + 1) * 128],
                                                                   h2T[:, kc, :], start=(kc == 0), stop=(kc == 7)),
                      reads=[w_in_sb.d[kc]] + h2T.all, writes=ps.all)
            r = rl[fc % 2]
            fw.op("act", lambda e, ps=ps, r=r: e.activation(r[:, :], ps[:, 0:CH], AF.Relu), reads=ps.all, writes=r.all)
            fw.op("dve", lambda e, fc=fc, r=r: e.tensor_tensor(uT[:, fc, :], r[:, :], r[:, :], ALU.mult),
                  reads=r.all, writes=uT.all)
        for n in range(NT):
            for half in range(2):
                ps = fw.psum()
                for fc in range(32):
                    fw.op("pe", lambda e, fc=fc, n=n, half=half, ps=ps: e.matmul(
                        ps[:, :], uT[:, fc, n * 128:(n + 1) * 128], w_o2_sb[:, fc, half * 512:(half + 1) * 512],
                        start=(fc == 0), stop=(fc == 31)), reads=uT.all + [w_o2_sb.d[fc // 4]], writes=ps.all)
                sl = slice(half * 512, (half + 1) * 512)
                fw.op("dve", lambda e, ps=ps, sl=sl: e.tensor_tensor(tmpf[:, :], ps[:, :], g2b[:, sl], ALU.mult),
                      reads=ps.all + g2b.all, writes=tmpf.all)
                fw.op("dve", lambda e, n=n, sl=sl: e.tensor_tensor(xb[:, n, sl], tmpf[:, :], xb[:, n, sl], ALU.add),
                      reads=tmpf.all + xb.all, writes=xb.all)
        fw.dma("sp", y[t0:t0 + CH, :].rearrange("(n p) d -> p n d", p=128), xb[:, :, :], reads=xb.all, is_output=True)
    fw.finish()
    return nc, fw


TWO_PI = 2.0 * math.pi
MAGIC = 12582912.0


class _V128:
    def __init__(self, b):
        self.b = b
        self.all = b.all

    def __getitem__(self, idx):
        return self.b.t[:, :, 0:128][idx]


def front_end(fw, x, cT, w_ada, b_cols, g_cols, ident, S, hT, wtmp, xbufs, xn, idb):
    ct = fw.sbuf([128, 8], F32, name="ct")
    fw.dma("sp", ct[:, :], cT[:, :], writes=ct.all)
    bcols = fw.sbuf([128, 48], F32, name="bcols")
    fw.dma("sp", bcols[:, :], b_cols[:, :], writes=bcols.all)
    gcols = fw.sbuf([128, 8], F32, name="gcols")
    fw.dma("sp", gcols[:, :], g_cols[:, :], writes=gcols.all)
    sc = fw.sbuf([128, 8], F32, name="sc")
    fw.op("act", lambda e: e.activation(sc[:, :], ct[:, :], AF.Silu), reads=ct.all, writes=sc.all)
    modc = fw.sbuf([128, 16], F32, name="modc")
    mod_cols(fw, sc, w_ada, bcols, 0, 16, modc, [_V128(wtmp[0]), _V128(wtmp[1])])
    geff = fw.sbuf([128, 8], F32, name="geff")
    fw.op("dve", lambda e: e.scalar_tensor_tensor(geff[:, :], modc[:, 8:16], 1.0, gcols[:, :], ALU.add, ALU.mult),
          reads=modc.all + gcols.all, writes=geff.all)
    ss = fw.sbuf([128, 4], F32, name="fe_ss")
    for tt in range(S // 128):
        xb = xbufs[tt % 2]
        fw.dma("sp", xb[:, :], x[tt * 128:(tt + 1) * 128, :], writes=xb.all)
        fw.op("dve", lambda e: e.memset(ss[:, :], 0.0), writes=ss.all)
        fw.op("act", lambda e, xb=xb: e.activation(xn[:, :], xb[:, :], AF.Square, accum_out=ss[:, 0:1]),
              reads=xb.all, writes=xn.all + ss.all)
        fw.op("dve", lambda e: e.tensor_scalar(ss[:, 1:2], ss[:, 0:1], 1.0 / D, EPS, ALU.mult, ALU.add),
              reads=ss.all, writes=ss.all)
        fw.op("act", lambda e: e.activation(ss[:, 1:2], ss[:, 1:2], AF.Sqrt), reads=ss.all, writes=ss.all)
        fw.op("dve", lambda e: e.reciprocal(ss[:, 2:3], ss[:, 1:2]), reads=ss.all, writes=ss.all)
        fw.op("dve", lambda e, xb=xb: e.tensor_scalar(xn[:, :], xb[:, :], ss[:, 2:3], None, ALU.mult),
              reads=xb.all + ss.all, writes=xn.all)
        ps = fw.psum_banks[tt % 2]
        psb = ps.t[:, :].bitcast(BF16)
        for kc in range(8):
            fw.op("pe", lambda e, kc=kc, psb=psb: e.transpose(psb[:, kc * 128:(kc + 1) * 128],
                                                             xn[:, kc * 128:(kc + 1) * 128], idb[:, :]),
                  reads=xn.all + idb.all, writes=ps.all)
        hd = [hT.d[tt // 4]]
        for kc in range(8):
            fw.op("act", lambda e, kc=kc, psb=psb, tt=tt: e.activation(
                hT[:, kc, tt * 128:(tt + 1) * 128], psb[:, kc * 128:(kc + 1) * 128], AF.Identity,
                bias=modc[:, kc:kc + 1], scale=geff[:, kc:kc + 1]),
                reads=ps.all + modc.all + geff.all, writes=hd)


def rope_tables(fw, pos, inv, S):
    NT = S // 128
    pi = fw.sbuf([128, NT], I32, name="posi")
    fw.dma("sp", pi[:, :], pos[:, :], writes=pi.all)
    invt = fw.sbuf([128, 8], F32, name="inv")
    fw.dma("sp", invt[:, :], inv[:, :], writes=invt.all)
    pf = fw.sbuf([128, NT], F32, name="posf")
    fw.op("dve", lambda e: e.tensor_copy(pf[:, :], pi[:, :]), reads=pi.all, writes=pf.all)
    ang = fw.sbuf([128, NT, 8], F32, name="ang")
    fw.op("dve", lambda e: e.tensor_tensor(ang[:, :, :], pf[:, :].unsqueeze(2).to_broadcast([128, NT, 8]),
                                           invt[:, :].unsqueeze(1).to_broadcast([128, NT, 8]), ALU.mult),
          reads=pf.all + invt.all, writes=ang.all)
    outs = []
    kk = fw.sbuf([128, NT, 8], F32, name="kk")
    for nm, shift in (("sin", 0.0), ("cos", math.pi / 2)):
        a2 = fw.sbuf([128, NT, 8], F32, name="a2" + nm)
        fw.op("dve", lambda e, a2=a2, shift=shift: e.tensor_scalar(a2[:, :, :], ang[:, :, :], shift, None, ALU.add),
              reads=ang.all, writes=a2.all)
        fw.op("dve", lambda e, a2=a2: e.tensor_scalar(kk[:, :, :], a2[:, :, :], 1.0 / TWO_PI, MAGIC, ALU.mult, ALU.add),
              reads=a2.all, writes=kk.all)
        fw.op("dve", lambda e: e.tensor_scalar(kk[:, :, :], kk[:, :, :], MAGIC, None, ALU.subtract),
              reads=kk.all, writes=kk.all)
        fw.op("dve", lambda e, a2=a2: e.scalar_tensor_tensor(a2[:, :, :], kk[:, :, :], -TWO_PI, a2[:, :, :], ALU.mult, ALU.add),
              reads=kk.all + a2.all, writes=a2.all)
        fw.op("dve", lambda e, a2=a2: e.tensor_scalar(a2[:, :, :], a2[:, :, :], 3.1415925, -3.1415925, ALU.min, ALU.max),
              reads=a2.all, writes=a2.all)
        tb = fw.sbuf([128, NT, 8], F32, name=nm)
        fw.op("act", lambda e, a2=a2, tb=tb: e.activation(tb[:, :, :], a2[:, :, :], AF.Sin), reads=a2.all, writes=tb.all)
        outs.append(tb)
    return outs[1], outs[0]


def norm_rope(fw, ps_ap, nh, gains, cosT, sinT, tt, qn, qb, sq, st, rope_mask, ps_deps, tmp):
    fw.op("act", lambda e: e.activation(sq[:, 0:nh * 64], ps_ap, AF.Square), reads=ps_deps, writes=sq.all)
    fw.op("dve", lambda e: e.tensor_reduce(st[:, 0:nh], sq[:, 0:nh * 64].rearrange("p (h d) -> p h d", h=nh), AX.X, ALU.add),
          reads=sq.all, writes=st.all)
    fw.op("dve", lambda e: e.tensor_scalar(st[:, nh:2 * nh], st[:, 0:nh], 1.0 / 64, EPS, ALU.mult, ALU.add),
          reads=st.all, writes=st.all)
    fw.op("act", lambda e: e.activation(st[:, nh:2 * nh], st[:, nh:2 * nh], AF.Sqrt), reads=st.all, writes=st.all)
    fw.op("dve", lambda e: e.reciprocal(st[:, 2 * nh:3 * nh], st[:, nh:2 * nh]), reads=st.all, writes=st.all)
    fw.op("dve", lambda e: e.tensor_tensor(qn[:, 0:nh, :], ps_ap.rearrange("p (h d) -> p h d", h=nh),
                                           st[:, 2 * nh:3 * nh].unsqueeze(2).to_broadcast([128, nh, 64]), ALU.mult),
          reads=ps_deps + st.all, writes=qn.all)
    fw.op("dve", lambda e: e.tensor_tensor(qn[:, 0:nh, :], qn[:, 0:nh, :], gains[:, 0:nh, :], ALU.mult),
          reads=qn.all + gains.all, writes=qn.all)
    fw.op("act", lambda e: e.activation(qb[:, 0:nh, :], qn[:, 0:nh, :], AF.Copy), reads=qn.all, writes=qb.all)
    if rope_mask is not None:
        r0, r1 = rope_mask
        n = r1 - r0
        c = cosT[:, tt, :].unsqueeze(1).to_broadcast([128, n, 8])
        s = sinT[:, tt, :].unsqueeze(1).to_broadcast([128, n, 8])
        x1 = qn[:, r0:r1, 0:8]
        x2 = qn[:, r0:r1, 8:16]
        T = lambda i: tmp[:, i, 0:n, :]
        rd = qn.all + cosT.all + sinT.all
        fw.op("dve", lambda e: e.tensor_tensor(T(0), x1, c, ALU.mult), reads=rd, writes=tmp.all)
        fw.op("dve", lambda e: e.tensor_tensor(T(1), x2, s, ALU.mult), reads=rd, writes=tmp.all)
        fw.op("dve", lambda e: e.tensor_tensor(T(2), x2, c, ALU.mult), reads=rd, writes=tmp.all)
        fw.op("dve", lambda e: e.tensor_tensor(T(3), x1, s, ALU.mult), reads=rd, writes=tmp.all)
        fw.op("dve", lambda e: e.tensor_tensor(qb[:, r0:r1, 0:8], T(0), T(1), ALU.subtract), reads=tmp.all, writes=qb.all)
        fw.op("dve", lambda e: e.tensor_tensor(qb[:, r0:r1, 8:16], T(2), T(3), ALU.add), reads=tmp.all, writes=qb.all)


def build_diff(NH=4, S=4096, lam_init=0.5):
    nc = bass.Bass("TRN2", target_bir_lowering=False)
    dt = lambda name, shape, kind="ExternalInput", dtype=F32: nc.dram_tensor(name, shape, dtype, kind=kind).ap()
    x = dt("x", [S, D])
    cT = dt("cT", [128, 8])
    w_ada = dt("w_ada", [D, 6 * D])
    b_cols = dt("b_cols", [128, 48])
    g_cols = dt("g_cols", [128, 8])
    pos = dt("pos", [128, S // 128], dtype=I32)
    inv = dt("inv", [128, 8])
    w_in = dt("w_in", [D, NH * 384])
    gains_d = dt("gains", [128, 4 * 64])
    lam_d = dt("lam", [128, 4 * 64])
    sg_d = dt("sg", [128, 1])
    ident = dt("ident", [128, 128])
    tri_d = dt("tri", [128, 128])
    oT = dt("oT", [NH * 128, S], kind="ExternalOutput")

    fw = FW(nc)
    fw.init_psum()
    NT = S // 128
    NJ = S // 512
    PB = fw.psum_banks

    idb = fw.sbuf([128, 128], BF16, name="idb")
    fw.dma("pool", idb[:, :], ident[:, :], writes=idb.all)
    trib = fw.sbuf([128, 128], BF16, name="trib")
    fw.dma("pool", trib[:, :], tri_d[:, :], writes=trib.all)
    onesb = fw.sbuf([128, 128], BF16, name="onesb")
    fw.op("dve", lambda e: e.memset(onesb[:, :], 1.0), writes=onesb.all)
    onesf = fw.sbuf([128, 128], F32, name="onesf")
    fw.op("dve", lambda e: e.memset(onesf[:, :], 1.0), writes=onesf.all)
    gains = fw.sbuf([128, 4, 64], F32, name="gains")
    fw.dma("sp", gains[:, :, :], gains_d.rearrange("p (h d) -> p h d", h=4), writes=gains.all)
    sgl = fw.sbuf([128, 1], F32, name="sgl")
    fw.dma("sp", sgl[:, :], sg_d[:, :], writes=sgl.all)
    fw.op("dve", lambda e: e.tensor_scalar(sgl[:, :], sgl[:, :], float(1.0 - lam_init), None, ALU.mult),
          reads=sgl.all, writes=sgl.all)
    lamt = fw.sbuf([128, 4, 64], F32, name="lamt")
    fw.dma("sp", lamt[:, :, :], lam_d.rearrange("p (h d) -> p h d", h=4), writes=lamt.all)
    lamp = fw.sbuf([128, 2, 64], F32, name="lamp")
    fw.op("dve", lambda e: e.tensor_tensor(lamp[:, :, :], lamt[:, 0:2, :], lamt[:, 2:4, :], ALU.mult),
          reads=lamt.all, writes=lamp.all)
    lams = fw.sbuf([128, 4], F32, name="lams")
    fw.op("dve", lambda e: e.tensor_reduce(lams[:, 0:2], lamp[:, :, :], AX.X, ALU.add), reads=lamp.all, writes=lams.all)
    fw.op("act", lambda e: e.activation(lams[:, 0:2], lams[:, 0:2], AF.Exp), reads=lams.all, writes=lams.all)
    fw.op("dve", lambda e: e.tensor_tensor(lams[:, 2:3], lams[:, 1:2], lams[:, 0:1], ALU.subtract), reads=lams.all, writes=lams.all)
    fw.op("dve", lambda e: e.tensor_scalar(lams[:, 3:4], lams[:, 2:3], float(-lam_init), None, ALU.add),
          reads=lams.all, writes=lams.all)
    nlam = lams

    cosT, sinT = rope_tables(fw, pos, inv, S)

    hT = fw.sbuf([128, 8, S], BF16, nreg=NJ, name="hT")
    wtmp = [fw.sbuf([128, 8, 512], F32, name="wtmp") for _ in range(2)]
    xbufs = [fw.sbuf([128, D], F32, name="xb") for _ in range(2)]
    xn = fw.sbuf([128, D], BF16, name="xn")
    front_end(fw, x, cT, w_ada, b_cols, g_cols, ident, S, hT, wtmp, xbufs, xn, idb)

    qkT = fw.view(wtmp[0].t[:, :, :].rearrange("p a b -> p (a b)").bitcast(BF16).rearrange("p (c t) -> p c t", c=2),
                  [wtmp[0]], nreg=NJ)
    Vtok = fw.view(wtmp[1].t[:, :, :].rearrange("p a b -> p (a b)").bitcast(BF16).rearrange("p (n d) -> p n d", d=128),
                   [wtmp[1]], nreg=NJ)
    w_sb = [fw.sbuf([128, 8, 384], BF16, name="w_sb") for _ in range(2)]
    qn = fw.sbuf([128, 4, 64], F32, name="qn")
    qb = fw.sbuf([128, 4, 64], BF16, name="qb")
    sq = fw.sbuf([128, 256], F32, name="sq")
    st = fw.sbuf([128, 12], F32, name="st")
    rtmp = fw.sbuf([128, 4, 4, 8], F32, name="rtmp")
    Pt = [fw.sbuf([128, 512], BF16, name="P") for _ in range(3)]
    fa = [fw.sbuf([128, 512], F32, name="fa") for _ in range(4)]
    ob = [fw.sbuf([128, 512], F32, name="ob") for _ in range(2)]

    for h in range(NH):
        wb = w_sb[h % 2]
        fw.dma("pool", wb[:, :, :], w_in[:, h * 384:(h + 1) * 384].rearrange("(kc p) n -> p kc n", p=128), writes=wb.all)
        for tt in range(NT):
            ps = PB[4 + tt % 2]
            for kc in range(8):
                fw.op("pe", lambda e, kc=kc, tt=tt, ps=ps: e.matmul(ps[:, 0:384], hT[:, kc, tt * 128:(tt + 1) * 128],
                                                                   wb[:, kc, :], start=(kc == 0), stop=(kc == 7)),
                      reads=[hT.d[tt // 4]] + wb.all, writes=ps.all)
            norm_rope(fw, ps[:, 0:256], 4, gains, cosT, sinT, tt, qn, qb, sq, st, (0, 4), ps.all, rtmp)
            fw.op("act", lambda e, tt=tt, ps=ps: e.activation(Vtok[:, tt, :], ps[:, 256:384], AF.Copy),
                  reads=ps.all, writes=[Vtok.d[tt // 4]])
            pt = PB[6 + tt % 2]
            ptb = pt.t[:, :].bitcast(BF16)
            for i in range(2):
                fw.op("pe", lambda e, i=i, ptb=ptb: e.transpose(ptb[:, i * 128:(i + 1) * 128],
                                                               qb[:, 2 * i:2 * i + 2, :].rearrange("p h d -> p (h d)"), idb[:, :]),
                      reads=qb.all + idb.all, writes=pt.all)
            fw.op("dve", lambda e, tt=tt, ptb=ptb: e.tensor_copy(qkT[:, :, tt * 128:(tt + 1) * 128],
                                                                 ptb[:, 0:256].rearrange("p (c t) -> p c t", c=2)),
                  reads=pt.all, writes=[qkT.d[tt // 4]])
        np_ = 0
        for j in range(NJ):
            q0 = j * 512
            nkt = 4 * j + 4
            for c in range(2):
                cp = slice(64 * c, 64 * c + 64)
                A = PB[2 * c]
                C = PB[2 * c + 1]
                for kt in range(nkt):
                    r = kt - 4 * j
                    lo = 128 * r if r > 0 else 0
                    Sb = PB[4 + np_ % 3]
                    P = Pt[np_ % 3]
                    np_ += 1
                    fw.op("pe", lambda e, Sb=Sb, kt=kt, lo=lo: e.matmul(
                        Sb[:, lo:512], qkT[cp, 1, kt * 128:(kt + 1) * 128], qkT[cp, 0, q0 + lo:q0 + 512],
                        start=True, stop=True), reads=[qkT.d[kt // 4], qkT.d[j]], writes=Sb.all)
                    fw.op("act", lambda e, Sb=Sb, P=P, lo=lo: e.activation(P[:, lo:512], Sb[:, lo:512], AF.Exp, scale=0.125),
                          reads=Sb.all, writes=P.all)
                    if r >= 0:
                        fw.op("pool", lambda e, P=P, lo=lo: e.tensor_tensor(P[:, lo:lo + 128], P[:, lo:lo + 128], trib[:, :], ALU.mult),
                              reads=P.all + trib.all, writes=P.all)
                    fw.op("pe", lambda e, P=P, kt=kt, lo=lo, A=A: e.matmul(
                        A[:, lo:512], Vtok[:, kt, :], P[:, lo:512], start=(kt == 0), stop=(kt == nkt - 1)),
                        reads=P.all + [Vtok.d[kt // 4]], writes=A.all)
                    fw.op("pe", lambda e, P=P, kt=kt, lo=lo, C=C: e.matmul(
                        C[:, lo:512], onesb[:, :], P[:, lo:512], start=(kt == 0), stop=(kt == nkt - 1)),
                        reads=P.all + onesb.all, writes=C.all)
            f0, f1, f2, f3 = fa
            fw.op("dve", lambda e: e.reciprocal(f0[:, :], PB[1][:, :]), reads=PB[1].all, writes=f0.all)
            fw.op("dve", lambda e: e.tensor_tensor(f1[:, :], PB[0][:, :], f0[:, :], ALU.mult), reads=PB[0].all + f0.all, writes=f1.all)
            fw.op("dve", lambda e: e.reciprocal(f2[:, :], PB[3][:, :]), reads=PB[3].all, writes=f2.all)
            fw.op("dve", lambda e: e.tensor_tensor(f3[:, :], PB[2][:, :], f2[:, :], ALU.mult), reads=PB[2].all + f2.all, writes=f3.all)
            fw.op("dve", lambda e: e.scalar_tensor_tensor(f1[:, :], f3[:, :], nlam[:, 3:4], f1[:, :], ALU.mult, ALU.add),
                  reads=f3.all + f1.all + nlam.all, writes=f1.all)
            fw.op("act", lambda e: e.activation(f0[:, :], f1[:, :], AF.Square), reads=f1.all, writes=f0.all)
            ssb = PB[7]
            fw.op("pe", lambda e: e.matmul(ssb[:, :], onesf[:, :], f0[:, :], start=True, stop=True),
                  reads=onesf.all + f0.all, writes=ssb.all)
            fw.op("dve", lambda e: e.tensor_scalar(f2[:, :], ssb[:, :], 1.0 / 128, EPS, ALU.mult, ALU.add),
                  reads=ssb.all, writes=f2.all)
            fw.op("act", lambda e: e.activation(f2[:, :], f2[:, :], AF.Sqrt), reads=f2.all, writes=f2.all)
            fw.op("dve", lambda e: e.reciprocal(f3[:, :], f2[:, :]), reads=f2.all, writes=f3.all)
            o = ob[j % 2]
            fw.op("dve", lambda e, o=o: e.scalar_tensor_tensor(o[:, :], f1[:, :], sgl[:, 0:1], f3[:, :], ALU.mult, ALU.mult),
                  reads=f1.all + f3.all + sgl.all, writes=o.all)
            fw.dma("sp", oT[h * 128:(h + 1) * 128, q0:q0 + 512], o[:, :], reads=o.all, is_output=True)
    fw.finish()
    return nc, fw


def build_nsa(NG=2, S=4096, stop=99, sub=99):
    nc = bass.Bass("TRN2", target_bir_lowering=False)
    dt = lambda name, shape, kind="ExternalInput", dtype=F32: nc.dram_tensor(name, shape, dtype, kind=kind).ap()
    NT = S // 128
    NJ = S // 512
    NCT = S // 2048
    NCP = NCT * 128
    NSEL = S // 64
    x = dt("x", [S, D])
    cT = dt("cT", [128, 8])
    w_ada = dt("w_ada", [D, 6 * D])
    b_cols = dt("b_cols", [128, 48])
    g_cols = dt("g_cols", [128, 8])
    pos = dt("pos", [128, NT], dtype=I32)
    inv = dt("inv", [128, 8])
    ident = dt("ident", [128, 128])
    tri_d = dt("tri", [128, 256])
    w_tok = dt("w_tok", [D, NG * 512])
    w_gate = dt("w_gate", [D, NG * 12])
    w_feat = dt("w_feat", [D, NG * 128])
    gains_d = dt("gains", [128, 6 * 64])
    bgate_d = dt("bgate", [12, NG])
    kg0_d = dt("kg0", [64, 1])
    peT_d = dt("peT", [128, 32])
    w1_d = dt("w1", [128, 32 * 128])
    w2_d = dt("w2", [128, 128])
    E_d = dt("E", [64, S])
    mb_d = dt("mb", [128, S])
    bb_d = dt("bb", [128, 248 + 256])
    vf_d = dt("vf", [128, 2 * 126])
    selE_d = dt("selE", [12, 12 * 64])
    hT_d = nc.dram_tensor("hT_scratch", [128, 8, S], BF16, kind="Internal").ap()
    oT = dt("oT", [NG * 256, S], kind="ExternalOutput")

    fw = FW(nc)
    fw.init_psum()
    PB = fw.psum_banks

    def cload(shape, dtype, src, q="sp", name=None):
        b = fw.sbuf(shape, dtype, name=name)
        idx = tuple(slice(None) for _ in shape)
        fw.dma(q, b[idx], src, writes=b.all)
        return b

    idb = cload([128, 128], BF16, ident[:, :], "pool", "idb")
    trib = cload([128, 256], BF16, tri_d[:, :], "pool", "trib")
    gains = cload([128, 6, 64], F32, gains_d.rearrange("p (h d) -> p h d", h=6), name="gains")
    bgate = cload([12, NG], F32, bgate_d[:, :], name="bgate")
    kg0 = cload([64, 1], F32, kg0_d[:, :], name="kg0")
    peT = cload([128, 32], BF16, peT_d[:, :], "pool", "peT")
    W1 = cload([128, 32, 128], BF16, w1_d.rearrange("p (j n) -> p j n", j=32), "pool", "W1")
    W2 = cload([128, 128], BF16, w2_d[:, :], "pool", "W2")
    MB = cload([128, S], BF16, mb_d[:, :], "pool", "MB")
    BB = cload([128, 504], F32, bb_d[:, :], name="BB")
    VF = cload([128, 252], F32, vf_d[:, :], name="VF")
    selE = cload([12, 12 * 64], F32, selE_d[:, :], name="selE")
    onesf = fw.sbuf([64, 64], F32, name="onesf")
    fw.op("dve", lambda e: e.memset(onesf[:, :], 1.0), writes=onesf.all)

    cosT, sinT = rope_tables(fw, pos, inv, S)

    hdram = Buf(hT_d, NJ)
    wtmp = [fw.sbuf([128, 8, 512], F32, name="wtmp") for _ in range(2)]
    xbufs = [fw.sbuf([128, D], F32, name="xb") for _ in range(2)]
    xn = fw.sbuf([128, D], BF16, name="xn")
    hch = [fw.sbuf([128, 8, 512], BF16, name="hch") for _ in range(2)]

    class _HT:
        def __init__(self):
            self.d = [None] * NJ
            for j in range(NJ):
                self.d[j] = hch[j % 2].d[0]

        def __getitem__(self, idx):
            p, kc, ts = idx
            j = ts.start // 512
            return hch[j % 2].t[p, kc, ts.start - j * 512:ts.stop - j * 512]
    ht_adapter = _HT()
    _orig_dma = fw.dma

    def front_with_flush():
        ct = fw.sbuf([128, 8], F32, name="ct")
        fw.dma("sp", ct[:, :], cT[:, :], writes=ct.all)
        bcols = fw.sbuf([128, 48], F32, name="bcols")
        fw.dma("sp", bcols[:, :], b_cols[:, :], writes=bcols.all)
        gcols = fw.sbuf([128, 8], F32, name="gcols")
        fw.dma("sp", gcols[:, :], g_cols[:, :], writes=gcols.all)
        sc = fw.sbuf([128, 8], F32, name="sc")
        fw.op("act", lambda e: e.activation(sc[:, :], ct[:, :], AF.Silu), reads=ct.all, writes=sc.all)
        modc = fw.sbuf([128, 16], F32, name="modc")
        mod_cols(fw, sc, w_ada, bcols, 0, 16, modc, [_V128(wtmp[0]), _V128(wtmp[1])])
        geff = fw.sbuf([128, 8], F32, name="geff")
        fw.op("dve", lambda e: e.scalar_tensor_tensor(geff[:, :], modc[:, 8:16], 1.0, gcols[:, :], ALU.add, ALU.mult),
              reads=modc.all + gcols.all, writes=geff.all)
        ss = fw.sbuf([128, 4], F32, name="fe_ss")
        for tt in range(NT):
            xb = xbufs[tt % 2]
            j = tt // 4
            hb = hch[j % 2]
            fw.dma("sp", xb[:, :], x[tt * 128:(tt + 1) * 128, :], writes=xb.all)
            fw.op("dve", lambda e: e.memset(ss[:, :], 0.0), writes=ss.all)
            fw.op("act", lambda e: e.activation(xn[:, :], xb[:, :], AF.Square, accum_out=ss[:, 0:1]),
                  reads=xb.all, writes=xn.all + ss.all)
            fw.op("dve", lambda e: e.tensor_scalar(ss[:, 1:2], ss[:, 0:1], 1.0 / D, EPS, ALU.mult, ALU.add),
                  reads=ss.all, writes=ss.all)
            fw.op("act", lambda e: e.activation(ss[:, 1:2], ss[:, 1:2], AF.Sqrt), reads=ss.all, writes=ss.all)
            fw.op("dve", lambda e: e.reciprocal(ss[:, 2:3], ss[:, 1:2]), reads=ss.all, writes=ss.all)
            fw.op("dve", lambda e: e.tensor_scalar(xn[:, :], xb[:, :], ss[:, 2:3], None, ALU.mult),
                  reads=xb.all + ss.all, writes=xn.all)
            ps = PB[tt % 2]
            psb = ps.t[:, :].bitcast(BF16)
            for kc in range(8):
                fw.op("pe", lambda e: e.transpose(psb[:, kc * 128:(kc + 1) * 128], xn[:, kc * 128:(kc + 1) * 128], idb[:, :]),
                      reads=xn.all + idb.all, writes=ps.all)
            lo = (tt % 4) * 128
            for kc in range(8):
                fw.op("act", lambda e: e.activation(hb[:, kc, lo:lo + 128], psb[:, kc * 128:(kc + 1) * 128], AF.Identity,
                                                    bias=modc[:, kc:kc + 1], scale=geff[:, kc:kc + 1]),
                      reads=ps.all + modc.all + geff.all, writes=hb.all)
            if tt % 4 == 3:
                fw.dma("sp", hT_d[:, :, j * 512:(j + 1) * 512], hb[:, :, :], reads=hb.all, writes=[hdram.d[j]])
    front_with_flush()
    if stop == 0:
        fw.dma("sp", oT[0:64, 0:512], xbufs[0][0:64, 0:512], reads=xbufs[0].all, is_output=True)
        fw.finish()
        return nc, fw

    QMraw = fw.view(wtmp[0].t[:, :, :].rearrange("p a b -> p (a b)").bitcast(BF16), [wtmp[0]])
    QMraw2 = fw.view(wtmp[1].t[:, :, :].rearrange("p a b -> p (a b)").bitcast(BF16), [wtmp[1]])
    if S == 4096:
        QM = [Buf(QMraw.t[:, 0:S], NJ), Buf(QMraw.t[:, S:2 * S], NJ), Buf(QMraw2.t[:, 0:S], NJ), Buf(QMraw2.t[:, S:2 * S], NJ)]
        for i, b in enumerate(QM):
            src = QMraw if i < 2 else QMraw2
            for d in b.d:
                d.rs = dict(src.d[0].rs)
    else:
        QM = [fw.sbuf([128, S], BF16, nreg=NJ, name="QM") for _ in range(4)]
    qcT = fw.sbuf([128, 2, S], BF16, nreg=NJ, name="qcT")
    KSE = fw.sbuf([128, S], BF16, nreg=NJ, name="KSE")
    fw.op("pool", lambda e: e.memset(KSE[64:128, :], 0.0), writes=KSE.all)
    fw.dma("pool", KSE[64:64 + min(64, NSEL), :], E_d[0:min(64, NSEL), :], writes=KSE.all)
    kwT = fw.sbuf([64, S], BF16, nreg=NJ, name="kwT")
    kvcT = fw.sbuf([128, S], BF16, nreg=NJ, name="kvcT")
    Vs = fw.sbuf([128, NT, 128], BF16, nreg=NJ, name="Vs")
    Vw = fw.sbuf([128, NT, 128], BF16, nreg=NJ, name="Vw")
    fw.op("pool", lambda e: e.memset(Vs[:, :, 64:128], 1.0), writes=Vs.all)
    fw.op("pool", lambda e: e.memset(Vw[:, :, 64:128], 1.0), writes=Vw.all)
    gT = fw.sbuf([12, S], F32, nreg=NJ, name="gT")
    w_tok_sb = fw.sbuf([128, 8, 512], BF16, name="w_tok")
    w_gate_sb = fw.sbuf([128, 8, 12], BF16, name="w_gate")
    w_feat_sb = fw.sbuf([128, 8, 128], BF16, name="w_feat")
    qn = fw.sbuf([128, 6, 64], F32, name="qn")
    qb = fw.sbuf([128, 6, 64], BF16, name="qb")
    qbc = fw.sbuf([128, 4, 64], BF16, name="qbc")
    sq = fw.sbuf([128, 384], F32, name="sq")
    st = fw.sbuf([128, 18], F32, name="st")
    rtmp = fw.sbuf([128, 4, 6, 8], F32, name="rtmp")
    kcmpT2 = fw.sbuf([128, NCP], BF16, name="kcmpT2")
    Vc = fw.sbuf([128, NCT, 128], BF16, name="Vc")
    hs = fw.sbuf([128, NCP], BF16, name="hs")
    cf = [fw.sbuf([64, NCP], F32, name="cf") for _ in range(3)]
    vcb = fw.sbuf([64, NCP], BF16, name="vcb")
    cbias = fw.sbuf([128, 2], F32, name="cbias")
    ee = fw.sbuf([128, 4, NCP], F32, name="ee")
    zz = fw.sbuf([128, 12], F32, name="zz")
    pcs = fw.sbuf([128, NCP], F32, name="pcs")
    imp = fw.sbuf([128, 4, 64], F32, name="imp")
    mx = fw.sbuf([128, 16], F32, name="mx")
    nmb = fw.sbuf([128, 64], BF16, name="nmb")
    Pt = [fw.sbuf([128, 512], BF16, name="P") for _ in range(3)]
    fa = [fw.sbuf([64, 512], F32, name="fa") for _ in range(3)]
    ob = [fw.sbuf([64, 512], F32, name="ob") for _ in range(2)]
    NS = min(64, NSEL)

    for g in range(NG):
        fw.dma("pool", w_tok_sb[:, :, :], w_tok[:, g * 512:(g + 1) * 512].rearrange("(kc p) n -> p kc n", p=128), writes=w_tok_sb.all)
        fw.dma("pool", w_gate_sb[:, :, :], w_gate[:, g * 12:(g + 1) * 12].rearrange("(kc p) n -> p kc n", p=128), writes=w_gate_sb.all)
        fw.dma("pool", w_feat_sb[:, :, :], w_feat[:, g * 128:(g + 1) * 128].rearrange("(kc p) n -> p kc n", p=128), writes=w_feat_sb.all)
        for j in range(NJ):
            hb = hch[j % 2]
            fw.dma("sp", hb[:, :, :], hT_d[:, :, j * 512:(j + 1) * 512], reads=[hdram.d[j]], writes=hb.all)
            ps = PB[2]
            for kc in range(8):
                fw.op("pe", lambda e: e.matmul(ps[:, :], w_feat_sb[:, kc, :], hb[:, kc, :], start=(kc == 0), stop=(kc == 7)),
                      reads=w_feat_sb.all + hb.all, writes=ps.all)
            fw.op("act", lambda e: e.activation(kvcT[:, j * 512:(j + 1) * 512], ps[:, :], AF.Copy), reads=ps.all, writes=[kvcT.d[j]])
            ps = PB[3]
            for kc in range(8):
                fw.op("pe", lambda e: e.matmul(ps[0:12, :], w_gate_sb[:, kc, :], hb[:, kc, :], start=(kc == 0), stop=(kc == 7)),
                      reads=w_gate_sb.all + hb.all, writes=ps.all)
            fw.op("act", lambda e: e.activation(gT[:, j * 512:(j + 1) * 512], ps[0:12, :], AF.Sigmoid, bias=bgate[:, g:g + 1]),
                  reads=ps.all + bgate.all, writes=[gT.d[j]])
            for t4 in range(4):
                tt = j * 4 + t4
                ps = PB[4 + tt % 2]
                for kc in range(8):
                    fw.op("pe", lambda e: e.matmul(ps[:, :], hb[:, kc, t4 * 128:(t4 + 1) * 128], w_tok_sb[:, kc, :],
                                                   start=(kc == 0), stop=(kc == 7)), reads=hb.all + w_tok_sb.all, writes=ps.all)
                norm_rope(fw, ps[:, 0:384], 6, gains, cosT, sinT, tt, qn, qb, sq, st, (0, 6), ps.all, rtmp)
                fw.op("act", lambda e: e.activation(qbc[:, :, :], qn[:, 0:4, :], AF.Copy), reads=qn.all, writes=qbc.all)
                fw.op("act", lambda e: e.activation(Vs[:, tt, 0:64], ps[:, 384:448], AF.Copy), reads=ps.all, writes=[Vs.d[j]])
                fw.op("act", lambda e: e.activation(Vw[:, tt, 0:64], ps[:, 448:512], AF.Copy), reads=ps.all, writes=[Vw.d[j]])
                pt = PB[6 + tt % 2]
                ptb = pt.t[:, :].bitcast(BF16)
                srcs = [qb[:, 0:2, :], qb[:, 2:4, :], qb[:, 4:6, :], qbc[:, 0:2, :], qbc[:, 2:4, :]]
                for i, s_ in enumerate(srcs):
                    fw.op("pe", lambda e: e.transpose(ptb[:, i * 128:(i + 1) * 128], s_.rearrange("p h d -> p (h d)"), idb[:, :]),
                          reads=qb.all + qbc.all + idb.all, writes=pt.all)
                tsl = slice(tt * 128, (tt + 1) * 128)
                fw.op("dve", lambda e: e.tensor_copy(QM[0][0:64, tsl], ptb[0:64, 0:128]), reads=pt.all, writes=[QM[0].d[j]])
                fw.op("dve", lambda e: e.tensor_copy(QM[1][0:64, tsl], ptb[64:128, 0:128]), reads=pt.all, writes=[QM[1].d[j]])
                fw.op("dve", lambda e: e.tensor_copy(QM[2][0:64, tsl], ptb[0:64, 128:256]), reads=pt.all, writes=[QM[2].d[j]])
                fw.op("dve", lambda e: e.tensor_copy(QM[3][0:64, tsl], ptb[64:128, 128:256]), reads=pt.all, writes=[QM[3].d[j]])
                fw.op("act", lambda e: e.activation(KSE[0:64, tsl], ptb[0:64, 256:384], AF.Copy), reads=pt.all, writes=[KSE.d[j]])
                fw.op("dve", lambda e: e.tensor_copy(kwT[0:64, tsl], ptb[64:128, 256:384]), reads=pt.all, writes=[kwT.d[j]])
                fw.op("dve", lambda e: e.tensor_copy(qcT[:, :, tsl], ptb[:, 384:640].rearrange("p (c t) -> p c t", c=2)),
                      reads=pt.all, writes=[qcT.d[j]])
        if stop == 1:
            fw.dma("sp", oT[0:64, 0:512], xbufs[0][0:64, 0:512], reads=xbufs[0].all + QM[0].all + QM[1].all + QM[2].all + QM[3].all + KSE.all + kwT.all + qcT.all + Vs.all + Vw.all + gT.all + kvcT.all, is_output=True)
            fw.finish()
            return nc, fw
        ncmp = S // 16 - 1
        for kind in range(2):
            dp = slice(64 * kind, 64 * kind + 64)
            psb_ = PB[0]
            for jj in range(32):
                fw.op("pe", lambda e: e.matmul(psb_[:, kind:kind + 1], W1[dp, jj, :], peT[dp, jj:jj + 1], start=(jj == 0), stop=(jj == 31)),
                      reads=W1.all + peT.all, writes=psb_.all)
            fw.op("dve", lambda e: e.tensor_copy(cbias[:, kind:kind + 1], psb_[:, kind:kind + 1]), reads=psb_.all, writes=cbias.all)
            psh = PB[1]
            for jj in range(32):
                kv3 = kvcT.t[dp, :].rearrange("p (c i) -> p c i", i=16)
                rhs = kv3[:, 0:ncmp, jj] if jj < 16 else kv3[:, 1:ncmp + 1, jj - 16]
                fw.op("pe", lambda e: e.matmul(psh[:, 0:ncmp], W1[dp, jj, :], rhs, start=(jj == 0), stop=(jj == 31)),
                      reads=W1.all + kvcT.all, writes=psh.all)
            fw.op("dve", lambda e: e.memset(hs[:, :], 0.0), writes=hs.all)
            fw.op("act", lambda e: e.activation(hs[:, 0:ncmp], psh[:, 0:ncmp], AF.Silu, bias=cbias[:, kind:kind + 1]),
                  reads=psh.all + cbias.all, writes=hs.all)
            pso = PB[2]
            fw.op("pe", lambda e: e.matmul(pso[0:64, 0:NCP], W2[:, 64 * kind:64 * kind + 64], hs[:, :], start=True, stop=True),
                  reads=W2.all + hs.all, writes=pso.all)
            if kind == 0:
                fw.op("act", lambda e: e.activation(cf[0][:, :], pso[0:64, 0:NCP], AF.Square), reads=pso.all, writes=cf[0].all)
                pss = PB[3]
                fw.op("pe", lambda e: e.matmul(pss[0:64, 0:NCP], onesf[:, :], cf[0][:, :], start=True, stop=True),
                      reads=onesf.all + cf[0].all, writes=pss.all)
                fw.op("dve", lambda e: e.tensor_scalar(cf[1][:, :], pss[0:64, 0:NCP], 1.0 / 64, EPS, ALU.mult, ALU.add),
                      reads=pss.all, writes=cf[1].all)
                fw.op("act", lambda e: e.activation(cf[1][:, :], cf[1][:, :], AF.Sqrt), reads=cf[1].all, writes=cf[1].all)
                fw.op("dve", lambda e: e.reciprocal(cf[2][:, :], cf[1][:, :]), reads=cf[1].all, writes=cf[2].all)
                fw.op("dve", lambda e: e.tensor_tensor(cf[0][:, :], pso[0:64, 0:NCP], cf[2][:, :], ALU.mult),
                      reads=pso.all + cf[2].all, writes=cf[0].all)
                fw.op("dve", lambda e: e.tensor_scalar(kcmpT2[0:64, :], cf[0][:, :], kg0[:, 0:1], None, ALU.mult),
                      reads=cf[0].all + kg0.all, writes=kcmpT2.all)
                fw.op("dve", lambda e: e.memset(kcmpT2[0:64, ncmp:NCP], 0.0), writes=kcmpT2.all)
                fw.op("dve", lambda e: e.tensor_copy(kcmpT2[64:128, :], kcmpT2[0:64, :]), reads=kcmpT2.all, writes=kcmpT2.all)
            else:
                fw.op("act", lambda e: e.activation(vcb[:, :], pso[0:64, 0:NCP], AF.Copy), reads=pso.all, writes=vcb.all)
                fw.op("dve", lambda e: e.memset(Vc[:, :, 64:128], 1.0), writes=Vc.all)
                for ct in range(NCT):
                    ptv = PB[4 + ct % 2]
                    ptvb = ptv.t[:, :].bitcast(BF16)
                    fw.op("pe", lambda e: e.transpose(ptvb[:, 0:64], vcb[:, ct * 128:(ct + 1) * 128], idb[0:64, 0:64]),
                          reads=vcb.all + idb.all, writes=ptv.all)
                    fw.op("dve", lambda e: e.tensor_copy(Vc[:, ct, 0:64], ptvb[:, 0:64]), reads=ptv.all, writes=Vc.all)
        if stop == 2:
            fw.dma("sp", oT[0:64, 0:512], xbufs[0][0:64, 0:512], reads=xbufs[0].all + kcmpT2.all + Vc.all, is_output=True)
            fw.finish()
            return nc, fw
        for tt in range(NT):
            j = tt // 4
            tsl = slice(tt * 128, (tt + 1) * 128)
            for pr in range(2):
                for hh in range(2):
                    h = 2 * pr + hh
                    psx = PB[4 + h % 2]
                    off = 0
                    cp = slice(64 * hh, 64 * hh + 64)
                    fw.op("pe", lambda e: e.matmul(psx[:, off:off + NCP], qcT[cp, pr, tsl], kcmpT2[cp, :], start=True, stop=True),
                          reads=[qcT.d[j]] + kcmpT2.all, writes=psx.all)
                    fw.op("act", lambda e: e.activation(ee[:, h, :], psx[:, off:off + NCP], AF.Exp, scale=0.125),
                          reads=psx.all, writes=ee.all)
            b0 = 248 - 8 * tt
            fw.op("dve", lambda e: e.tensor_tensor(ee[:, :, :], ee[:, :, :], BB[:, b0:b0 + NCP].unsqueeze(1).to_broadcast([128, 4, NCP]), ALU.mult),
                  reads=ee.all + BB.all, writes=ee.all)
            fw.op("dve", lambda e: e.tensor_reduce(zz[:, 0:4], ee[:, :, :], AX.X, ALU.add), reads=ee.all, writes=zz.all)
            fw.op("dve", lambda e: e.tensor_scalar(zz[:, 4:8], zz[:, 0:4], 1e-30, None, ALU.max), reads=zz.all, writes=zz.all)
            fw.op("dve", lambda e: e.reciprocal(zz[:, 8:12], zz[:, 4:8]), reads=zz.all, writes=zz.all)
            fw.op("dve", lambda e: e.tensor_tensor(ee[:, :, :], ee[:, :, :], zz[:, 8:12].unsqueeze(2).to_broadcast([128, 4, NCP]), ALU.mult),
                  reads=ee.all + zz.all, writes=ee.all)
            fw.op("dve", lambda e: e.tensor_reduce(pcs[:, :], ee[:, :, :].rearrange("p h c -> p c h"), AX.X, ALU.add),
                  reads=ee.all, writes=pcs.all)
            if sub == 0:
                continue
            nsb = NCP // 4
            fw.op("dve", lambda e: e.memset(imp[:, 0, :], 0.0), writes=imp.all)
            fw.op("dve", lambda e: e.tensor_reduce(imp[:, 0, 0:nsb], pcs[:, :].rearrange("p (s i) -> p s i", i=4), AX.X, ALU.add),
                  reads=pcs.all, writes=imp.all)
            fw.op("dve", lambda e: e.tensor_tensor(imp[:, 0, 1:nsb], imp[:, 0, 1:nsb],
                                                   pcs[:, :].rearrange("p (s i) -> p s i", i=4)[:, 0:nsb - 1, 3], ALU.add),
                  reads=pcs.all + imp.all, writes=imp.all)
            v0 = 62 - 2 * tt
            fw.op("dve", lambda e: e.scalar_tensor_tensor(imp[:, 1, :], imp[:, 0, :], 1.0, VF[:, v0:v0 + 64], ALU.add, ALU.mult),
                  reads=imp.all + VF.all, writes=imp.all)
            fw.op("dve", lambda e: e.tensor_scalar(imp[:, 1, :], imp[:, 1, :], -1.0, None, ALU.add), reads=imp.all, writes=imp.all)
            fw.op("dve", lambda e: e.scalar_tensor_tensor(imp[:, 1, :], VF[:, 126 + v0:126 + v0 + 64], 1e6, imp[:, 1, :], ALU.mult, ALU.max),
                  reads=imp.all + VF.all, writes=imp.all)
            fw.op("dve", lambda e: e.memset(imp[:, 1, 0:1], 1e6), writes=imp.all)
            if sub == 1:
                continue
            fw.op("dve", lambda e: e.max(mx[:, 0:8], imp[:, 1, :]), reads=imp.all, writes=mx.all)
            fw.op("dve", lambda e: e.match_replace(imp[:, 2, :], mx[:, 0:8], imp[:, 1, :], -2.0), reads=imp.all + mx.all, writes=imp.all)
            fw.op("dve", lambda e: e.max(mx[:, 8:16], imp[:, 2, :]), reads=imp.all, writes=mx.all)
            if sub == 2:
                continue
            fw.op("dve", lambda e: e.tensor_reduce(zz[:, 0:1], mx[:, 8:16], AX.X, ALU.min), reads=mx.all, writes=zz.all)
            fw.op("dve", lambda e: e.tensor_scalar(imp[:, 3, :], imp[:, 1, :], zz[:, 0:1], None, ALU.is_ge),
                  reads=imp.all + zz.all, writes=imp.all)
            fw.op("dve", lambda e: e.tensor_scalar(nmb[:, :], imp[:, 3, :], 30000.0, -30000.0, ALU.mult, ALU.add), reads=imp.all, writes=nmb.all)
            if sub == 3:
                continue
            ptn = PB[6 + tt % 2]
            ptnb = ptn.t[:, :].bitcast(BF16)
            fw.op("pe", lambda e: e.transpose(ptnb[0:64, 0:128], nmb[:, :], idb[:, :]), reads=nmb.all + idb.all, writes=ptn.all)
            for h in range(4):
                eng = "dve"
                if eng == "act":
                    fw.op("act", lambda e: e.activation(QM[h][64:128, tsl], ptnb[0:64, 0:128], AF.Copy), reads=ptn.all, writes=[QM[h].d[j]])
                else:
                    fw.op("dve", lambda e: e.tensor_copy(QM[h][64:128, tsl], ptnb[0:64, 0:128]), reads=ptn.all, writes=[QM[h].d[j]])
        if stop == 3:
            fw.dma("sp", oT[0:64, 0:512], xbufs[0][0:64, 0:512], reads=xbufs[0].all + QM[0].all + QM[1].all + QM[2].all + QM[3].all, is_output=True)
            fw.finish()
            return nc, fw
        np_ = 0
        for h in range(4):
            pr, hh = h // 2, h % 2
            cp = slice(64 * hh, 64 * hh + 64)
            for j in range(NJ):
                q0 = j * 512
                qs = slice(q0, q0 + 512)
                A = PB[0]
                cts = [ct for ct in range(NCT) if 128 * ct <= 32 * j + 30]
                for i, ct in enumerate(cts):
                    Sb = PB[3 + np_ % 3]
                    P = Pt[np_ % 3]
                    np_ += 1
                    fw.op("pe", lambda e: e.matmul(Sb[:, :], kcmpT2[cp, ct * 128:(ct + 1) * 128], qcT[cp, pr, qs], start=True, stop=True),
                          reads=kcmpT2.all + [qcT.d[j]], writes=Sb.all)
                    fw.op("act", lambda e: e.activation(P[:, :], Sb[:, :], AF.Exp, scale=0.125), reads=Sb.all, writes=P.all)
                    m0 = q0 - 2048 * ct
                    fw.op("pool", lambda e: e.tensor_tensor(P[:, :], P[:, :], MB[:, m0:m0 + 512], ALU.mult), reads=P.all + MB.all, writes=P.all)
                    fw.op("pe", lambda e: e.matmul(A[:, :], Vc[:, ct, :], P[:, :], start=(i == 0), stop=(i == len(cts) - 1)),
                          reads=P.all + Vc.all, writes=A.all)
                for br in range(2):
                    A = PB[1 + br]
                    if br == 0:
                        kts = list(range(0, 4 * j + 4))
                    elif j == 0:
                        kts = [0, 1, 2, 3]
                    else:
                        kts = [4 * j - 1, 4 * j - 4, 4 * j - 3, 4 * j - 2, 4 * j, 4 * j + 1, 4 * j + 2, 4 * j + 3]
                    for i, kt in enumerate(kts):
                        r = kt - 4 * j
                        lo = 128 * r if r > 0 else 0
                        hi = 512 if (r >= -1 or br == 0) else 128 * (5 + r)
                        Sb = PB[3 + np_ % 3]
                        P = Pt[np_ % 3]
                        np_ += 1
                        ks_ = slice(kt * 128, (kt + 1) * 128)
                        if br == 0:
                            fw.op("pe", lambda e: e.matmul(Sb[:, lo:hi], KSE[:, ks_], QM[h][:, q0 + lo:q0 + hi], start=True, stop=True),
                                  reads=[KSE.d[kt // 4], QM[h].d[j]], writes=Sb.all)
                        else:
                            fw.op("pe", lambda e: e.matmul(Sb[:, lo:hi], kwT[0:64, ks_], QM[h][0:64, q0 + lo:q0 + hi], start=True, stop=True),
                                  reads=[kwT.d[kt // 4], QM[h].d[j]], writes=Sb.all)
                        fw.op("act", lambda e: e.activation(P[:, lo:hi], Sb[:, lo:hi], AF.Exp, scale=0.125), reads=Sb.all, writes=P.all)
                        if r >= 0:
                            fw.op("pool", lambda e: e.tensor_tensor(P[:, lo:lo + 128], P[:, lo:lo + 128], trib[:, 0:128], ALU.mult),
                                  reads=P.all + trib.all, writes=P.all)
                        elif br == 1:
                            fw.op("pool", lambda e: e.tensor_tensor(P[:, hi - 128:hi], P[:, hi - 128:hi], trib[:, 128:256], ALU.mult),
                                  reads=P.all + trib.all, writes=P.all)
                        V = Vs if br == 0 else Vw
                        fw.op("pe", lambda e: e.matmul(A[:, lo:hi], V[:, kt, :], P[:, lo:hi], start=(i == 0), stop=(i == len(kts) - 1)),
                              reads=P.all + [V.d[kt // 4]], writes=A.all)
                o = ob[j % 2]
                for b3 in range(3):
                    A = PB[b3]
                    f0, f1, f2 = fa
                    if b3 == 0:
                        fw.op("dve", lambda e: e.tensor_scalar(f0[:, :], A[64:128, :], 1e-30, None, ALU.max), reads=A.all, writes=f0.all)
                        fw.op("dve", lambda e: e.reciprocal(f0[:, :], f0[:, :]), reads=f0.all, writes=f0.all)
                    else:
                        fw.op("dve", lambda e: e.reciprocal(f0[:, :], A[64:128, :]), reads=A.all, writes=f0.all)
                    fw.op("dve", lambda e: e.tensor_tensor(f1[:, :], A[0:64, :], f0[:, :], ALU.mult), reads=A.all + f0.all, writes=f1.all)
                    G = PB[6 + b3 % 2]
                    col = (h * 3 + b3) * 64
                    fw.op("pe", lambda e: e.matmul(G[0:64, :], selE[:, col:col + 64], gT[:, qs], start=True, stop=True),
                          reads=selE.all + [gT.d[j]], writes=G.all)
                    if b3 == 0:
                        fw.op("dve", lambda e: e.tensor_tensor(o[:, :], G[0:64, :], f1[:, :], ALU.mult), reads=G.all + f1.all, writes=o.all)
                    else:
                        fw.op("dve", lambda e: e.tensor_tensor(f2[:, :], G[0:64, :], f1[:, :], ALU.mult), reads=G.all + f1.all, writes=f2.all)
                        fw.op("dve", lambda e: e.tensor_tensor(o[:, :], o[:, :], f2[:, :], ALU.add), reads=o.all + f2.all, writes=o.all)
                row = (g * 4 + h) * 64
                fw.dma("sp", oT[row:row + 64, qs], o[:, :], reads=o.all, is_output=True)
    fw.finish()
    return nc, fw


def nsa_consts(S):
    NSEL = S // 64
    tri = np.concatenate([np.triu(np.ones((128, 128), np.float32)), np.tril(np.ones((128, 128), np.float32), -1)], axis=1)
    E = (np.arange(S)[None, :] // 64 == np.arange(64)[:, None]).astype(np.float32)
    mb = (np.arange(S)[None, :] >= 16 * np.arange(128)[:, None] + 31).astype(np.float32)
    i = np.arange(504)[None, :]; p = np.arange(128)[:, None]
    bb = (16 * (i - 248) + 31 <= p).astype(np.float32)
    i = np.arange(126)[None, :]; hi = (p >= 64).astype(np.int64)
    valid = ((i - 62) <= hi).astype(np.float32)
    forced = (((i - 62) == hi) | ((i - 62) == hi - 1)).astype(np.float32)
    vf = np.concatenate([valid, forced], axis=1)
    selE = np.zeros((12, 12 * 64), np.float32)
    for r in range(12):
        selE[r, r * 64:(r + 1) * 64] = 1
    return dict(tri=tri, E=E, mb=mb, bb=bb, vf=vf, selE=selE, ident=np.eye(128, dtype=np.float32))

def nsa_group_inputs(groups, w_in, b_gate, q_gain, k_gain, pe_k, w_ck1, w_ck2, pe_v, w_cv1, w_cv2):
    wt, wg, wf, bg = [], [], [], []
    for g in groups:
        q = w_in[:, g * 256:(g + 1) * 256]
        def kv(i): return w_in[:, 1024 + i * 256 + g * 64: 1024 + i * 256 + (g + 1) * 64]
        kc, vc, ks, vs, kw, vw = [kv(i) for i in range(6)]
        wt.append(np.concatenate([q, ks, kw, vs, vw], axis=1))
        wg.append(w_in[:, 2560 + g * 12: 2560 + (g + 1) * 12])
        wf.append(np.concatenate([kc, vc], axis=1))
        bg.append(b_gate[g * 12:(g + 1) * 12])
    gains = np.concatenate([q_gain] * 4 + [k_gain[1], k_gain[2]])
    def w1l(w): return w.reshape(32, 64, 128).transpose(1, 0, 2).reshape(64, 32 * 128)
    return dict(w_tok=np.concatenate(wt, 1), w_gate=np.concatenate(wg, 1), w_feat=np.concatenate(wf, 1),
                gains=np.tile(gains[None, :], (128, 1)), bgate=np.stack(bg, 1), kg0=k_gain[0].reshape(64, 1),
                peT=np.concatenate([pe_k.T, pe_v.T], 0), w1=np.concatenate([w1l(w_ck1), w1l(w_cv1)], 0),
                w2=np.concatenate([w_ck2, w_cv2], 1))


DEPTH = 4
SEQ = 4096
BATCH = 4
_INV = (500000.0 ** (-np.arange(0, 16, 2, dtype=np.float32) / 16)).astype(np.float32)


def _c(a):
    return np.ascontiguousarray(a)


def _common(i, b, x_b, c, positions, w_ada, b_ada, g):
    return dict(x=_c(x_b), cT=_c(c[b].reshape(8, 128).T), w_ada=_c(w_ada[i]),
                b_cols=_c(b_ada[i].reshape(48, 128).T), g_cols=_c(g[i].reshape(8, 128).T),
                pos=_c(positions[b].reshape(SEQ // 128, 128).T.astype(np.int32)), inv=_c(np.tile(_INV[None, :], (128, 1))))


def _diff_wslice(w_in, h):
    return np.concatenate([w_in[:, (2 * h) * 64:(2 * h + 1) * 64], w_in[:, (2 * h + 1) * 64:(2 * h + 2) * 64],
                           w_in[:, 1024 + (2 * h) * 64:1024 + (2 * h + 1) * 64], w_in[:, 1024 + (2 * h + 1) * 64:1024 + (2 * h + 2) * 64],
                           w_in[:, 2048 + h * 128:2048 + (h + 1) * 128]], axis=1)


def kernel(x, c, positions, ln_mix_g, ln_mlp_g, w_ada, b_ada, w_mlp_in, w_mlp_out,
           nsa_w_in, nsa_b_gate, nsa_q_gain, nsa_k_gain, nsa_pe_k, nsa_w_ck1, nsa_w_ck2,
           nsa_pe_v, nsa_w_cv1, nsa_w_cv2, nsa_w_out,
           diff_w_in, diff_q_gain, diff_k_gain, diff_lq1, diff_lk1, diff_lq2, diff_lk2,
           diff_subln_g, diff_w_out):
    f32 = lambda a: np.asarray(a, dtype=np.float32)
    x = f32(x); c = f32(c); positions = np.asarray(positions)
    ln_mix_g = f32(ln_mix_g); ln_mlp_g = f32(ln_mlp_g); w_ada = f32(w_ada); b_ada = f32(b_ada)
    w_mlp_in = f32(w_mlp_in); w_mlp_out = f32(w_mlp_out)
    ident = np.eye(128, dtype=np.float32)
    cores = list(range(8))
    xs = [x[b] for b in range(BATCH)]
    consts = nsa_consts(SEQ)
    for i in range(DEPTH):
        j = i // 2
        in_maps = []
        if i % 2 == 0:
            nc, _ = build_nsa(NG=2, S=SEQ)
            for b in range(BATCH):
                for hh in range(2):
                    d = _common(i, b, xs[b], c, positions, w_ada, b_ada, ln_mix_g)
                    d.update(consts)
                    d.update(nsa_group_inputs([2 * hh, 2 * hh + 1], f32(nsa_w_in[j]), f32(nsa_b_gate[j]), f32(nsa_q_gain[j]),
                                              f32(nsa_k_gain[j]), f32(nsa_pe_k[j]), f32(nsa_w_ck1[j]), f32(nsa_w_ck2[j]),
                                              f32(nsa_pe_v[j]), f32(nsa_w_cv1[j]), f32(nsa_w_cv2[j])))
                    in_maps.append({k: _c(v) for k, v in d.items()})
            wout = f32(nsa_w_out[j])
        else:
            lam_init = 0.8 - 0.6 * math.exp(-0.3 * ((i + 1) - 1))
            nc, _ = build_diff(NH=4, S=SEQ, lam_init=lam_init)
            w_in = f32(diff_w_in[j])
            qg, kg = f32(diff_q_gain[j]), f32(diff_k_gain[j])
            for b in range(BATCH):
                for hh in range(2):
                    d = _common(i, b, xs[b], c, positions, w_ada, b_ada, ln_mix_g)
                    d["ident"] = ident
                    d["tri"] = np.triu(np.ones((128, 128), dtype=np.float32))
                    d["w_in"] = np.concatenate([_diff_wslice(w_in, h) for h in range(4 * hh, 4 * hh + 4)], axis=1)
                    d["gains"] = np.tile(np.concatenate([qg, qg, kg, kg])[None, :], (128, 1))
                    d["lam"] = np.tile(np.concatenate([f32(diff_lq1[j]), f32(diff_lq2[j]), f32(diff_lk1[j]), f32(diff_lk2[j])])[None, :], (128, 1))
                    d["sg"] = f32(diff_subln_g[j]).reshape(128, 1)
                    in_maps.append({k: _c(v) for k, v in d.items()})
            wout = f32(diff_w_out[j])
        res = run_bass_kernel_spmd(nc, in_maps, core_ids=cores)
        oTs = [res.results[k]["oT"] for k in range(8)]
        nc2, _ = build_mlp(TOK=SEQ // 2, CH=256)
        in_maps = []
        for b in range(BATCH):
            oT_full = np.concatenate([oTs[2 * b], oTs[2 * b + 1]], axis=0)
            for th in range(2):
                ts = slice(th * SEQ // 2, (th + 1) * SEQ // 2)
                in_maps.append(dict(x=_c(xs[b][ts]), oT=_c(oT_full[:, ts]), cT=_c(c[b].reshape(8, 128).T), w_ada=_c(w_ada[i]),
                                    b_cols=_c(b_ada[i].reshape(48, 128).T), b_row=_c(b_ada[i].reshape(1, -1)),
                                    g_cols=_c(ln_mlp_g[i].reshape(8, 128).T), w_out=_c(wout), w_in=_c(w_mlp_in[i]),
                                    w_o2=_c(w_mlp_out[i]), ident=ident))
        res = run_bass_kernel_spmd(nc2, in_maps, core_ids=cores)
        xs = [np.concatenate([res.results[2 * b]["y"], res.results[2 * b + 1]["y"]], axis=0) for b in range(BATCH)]
    return np.stack(xs, axis=0).astype(np.float32)
```
